# Optimizing a Trainium2 kernel written in Bass

```python
import math
import jax, jax.numpy as jnp
from jax import lax
import numpy as np

D_MODEL = 4096
BATCH = 1
SEQ = 8192
DEPTH = 2

HEAD_DIM = 128
N_HEADS_A = 16
N_KV_HEADS_A = 4
ATTN_WIDTH = N_HEADS_A * HEAD_DIM
KV_WIDTH = N_KV_HEADS_A * HEAD_DIM
IDX_HEADS = 16
IDX_DIM = 64
TOPK_MAX = 256
Q_BLOCK = 128
ROPE_THETA = 500000.0
ROPE_FRAC_DEN = 4
SGU_CHUNK = 128
SGU_GROUPS = 8
SGU_GROUP_DIM = 128
SGU_WIDTH = SGU_GROUPS * SGU_GROUP_DIM
POOL_WINDOWS = (2, 4, 8, 16)
POOL_GROUPS = 4
POOL_GROUP_DIM = 256
POOL_WIDTH = POOL_GROUPS * POOL_GROUP_DIM
D_FF = 11008
N_BRANCHES = 3
EPS = 1e-6
SPLITS = (ATTN_WIDTH, KV_WIDTH, KV_WIDTH, IDX_HEADS * IDX_DIM, IDX_DIM, IDX_HEADS,
          2 * SGU_WIDTH, POOL_WIDTH, N_BRANCHES * D_MODEL)
N_IN = sum(SPLITS)

kernel_name = "hybrid_dsa_sgu_pool_gated_macaron"


def rms_norm(x, g):
    xf = x.astype(jnp.float32)
    y = xf * lax.rsqrt(jnp.mean(xf * xf, axis=-1, keepdims=True) + EPS)
    return (y * g.astype(jnp.float32)).astype(x.dtype)


def swiglu(x, w_in, w_out):
    g, u = jnp.split(x @ w_in, 2, axis=-1)
    return (jax.nn.silu(g) * u) @ w_out


def partial_rope(x, positions):
    d = x.shape[-1]
    rd = d // ROPE_FRAC_DEN
    half = rd // 2
    inv_freq = ROPE_THETA ** (-(jnp.arange(0, rd, 2, dtype=jnp.float32) / rd))
    ang = positions.astype(jnp.float32)[..., None] * inv_freq
    cos = jnp.cos(ang)[:, :, None, :]
    sin = jnp.sin(ang)[:, :, None, :]
    xr = x[..., :rd].astype(jnp.float32)
    x1, x2 = xr[..., :half], xr[..., half:]
    rot = jnp.concatenate([x1 * cos - x2 * sin, x2 * cos + x1 * sin], axis=-1).astype(x.dtype)
    return jnp.concatenate([rot, x[..., rd:]], axis=-1)


def dsa_attention(q, k, v, iq, ik, iw):
    B, S = q.shape[0], q.shape[1]
    topk = min(TOPK_MAX, S // 4)
    nb = S // Q_BLOCK
    G = N_HEADS_A // N_KV_HEADS_A
    idx_scale = (IDX_HEADS ** -0.5) * (IDX_DIM ** -0.5)
    ikf = ik.astype(jnp.float32)
    key_pos = jnp.arange(S, dtype=jnp.int32)

    def to_blocks(a):
        return jnp.moveaxis(a.reshape((B, nb, Q_BLOCK) + a.shape[2:]), 1, 0)

    def one_block(args):
        qb, iqb, iwb, start = args
        qpos = start + jnp.arange(Q_BLOCK, dtype=jnp.int32)
        dots = jnp.einsum('bthd,bsd->bths', iqb.astype(jnp.float32), ikf)
        score = jnp.einsum('bth,bths->bts', iwb.astype(jnp.float32), jax.nn.relu(dots)) * idx_scale
        causal = key_pos[None, :] <= qpos[:, None]
        score = jnp.where(causal[None], score, -jnp.inf)
        _, idx = lax.top_k(score, topk)
        kg = jax.vmap(lambda kb, ib: kb[ib])(k, idx)
        vg = jax.vmap(lambda vb, ib: vb[ib])(v, idx)
        qg = qb.reshape(B, Q_BLOCK, N_KV_HEADS_A, G, HEAD_DIM)
        logits = jnp.einsum('btjgd,btnjd->btjgn', qg, kg).astype(jnp.float32) * (HEAD_DIM ** -0.5)
        valid = (idx <= qpos[None, :, None])[:, :, None, None, :]
        logits = jnp.where(valid, logits, -jnp.inf)
        p = jax.nn.softmax(logits, axis=-1).astype(v.dtype)
        o = jnp.einsum('btjgn,btnjd->btjgd', p, vg)
        return o.reshape(B, Q_BLOCK, ATTN_WIDTH)

    starts = jnp.arange(nb, dtype=jnp.int32) * Q_BLOCK
    out = lax.map(one_block, (to_blocks(q), to_blocks(iq), to_blocks(iw), starts))
    return jnp.moveaxis(out, 0, 1).reshape(B, S, ATTN_WIDTH)


def spatial_gating(z, ln_g, ln_b, w_s, b_s):
    B, S = z.shape[0], z.shape[1]
    z = jax.nn.gelu(z)
    u, v = jnp.split(z, 2, axis=-1)
    vf = v.astype(jnp.float32)
    mu = jnp.mean(vf, axis=-1, keepdims=True)
    var = jnp.mean(jnp.square(vf - mu), axis=-1, keepdims=True)
    v = ((vf - mu) * lax.rsqrt(var + EPS) * ln_g.astype(jnp.float32) + ln_b.astype(jnp.float32)).astype(z.dtype)
    nc = S // SGU_CHUNK
    v = v.reshape(B, nc, SGU_CHUNK, SGU_GROUPS, SGU_GROUP_DIM)
    mask = jnp.tril(jnp.ones((SGU_CHUNK, SGU_CHUNK), dtype=bool))
    w = jnp.where(mask, w_s, 0)
    s = jnp.einsum('gij,bcjgd->bcigd', w, v) + b_s.T[None, None, :, :, None]
    return u * s.reshape(B, S, SGU_WIDTH)


def multiscale_pool(c, w_pool, scale):
    B, S = c.shape[0], c.shape[1]
    cg = c.reshape(B, S, POOL_GROUPS, POOL_GROUP_DIM)
    cs = jnp.cumsum(cg.astype(jnp.float32), axis=1)
    cs = jnp.concatenate([jnp.zeros_like(cs[:, :1]), cs], axis=1)
    t = jnp.arange(S, dtype=jnp.int32)[:, None]
    win = jnp.array(POOL_WINDOWS, dtype=jnp.int32)[None, :]
    lo = jnp.maximum(t + 1 - win, 0)
    lower = cs[:, lo, jnp.arange(POOL_GROUPS)[None, :], :]
    count = (t + 1 - lo).astype(jnp.float32)
    mean = (cs[:, 1:] - lower) / count[None, :, :, None]
    y = (mean - cg.astype(jnp.float32)).astype(c.dtype)
    y = jnp.einsum('bsgc,gcd->bsgd', y, w_pool).reshape(B, S, POOL_WIDTH)
    return y * scale


def setup_inputs(seed: int = 0) -> dict:
    key = jax.random.key(seed)
    ks = jax.random.split(key, 24)
    L = DEPTH

    def normal(k, shape, s):
        return jax.random.normal(k, shape, jnp.float32) * s

    def gain(k, shape, s=0.02):
        return 1.0 + s * jax.random.normal(k, shape, jnp.float32)

    return {
        "x": normal(ks[0], (BATCH, SEQ, D_MODEL), 1.0),
        "positions": jnp.broadcast_to(jnp.arange(SEQ, dtype=jnp.int32)[None, :], (BATCH, SEQ)),
        "ffn1_norm": gain(ks[1], (L, D_MODEL)),
        "ffn1_w_in": normal(ks[2], (L, D_MODEL, 2 * D_FF), D_MODEL ** -0.5),
        "ffn1_w_out": normal(ks[3], (L, D_FF, D_MODEL), D_FF ** -0.5),
        "mix_norm": gain(ks[4], (L, D_MODEL)),
        "w_in": normal(ks[5], (L, D_MODEL, N_IN), D_MODEL ** -0.5),
        "sgu_ln_g": gain(ks[6], (L, SGU_WIDTH)),
        "sgu_ln_b": normal(ks[7], (L, SGU_WIDTH), 0.02),
        "sgu_w": normal(ks[8], (L, SGU_GROUPS, SGU_CHUNK, SGU_CHUNK), 0.5 * SGU_CHUNK ** -0.5),
        "sgu_b": gain(ks[9], (L, SGU_GROUPS, SGU_CHUNK), 0.1),
        "pool_w": normal(ks[10], (L, POOL_GROUPS, POOL_GROUP_DIM, POOL_GROUP_DIM), POOL_GROUP_DIM ** -0.5),
        "pool_scale": gain(ks[11], (L, POOL_WIDTH), 0.1),
        "proj_a": normal(ks[12], (L, ATTN_WIDTH, D_MODEL), ATTN_WIDTH ** -0.5),
        "proj_b": normal(ks[13], (L, SGU_WIDTH, D_MODEL), SGU_WIDTH ** -0.5),
        "proj_c": normal(ks[14], (L, POOL_WIDTH, D_MODEL), POOL_WIDTH ** -0.5),
        "w_out": normal(ks[15], (L, D_MODEL, D_MODEL), D_MODEL ** -0.5),
        "ffn2_norm": gain(ks[16], (L, D_MODEL)),
        "ffn2_w_in": normal(ks[17], (L, D_MODEL, 2 * D_FF), D_MODEL ** -0.5),
        "ffn2_w_out": normal(ks[18], (L, D_FF, D_MODEL), D_FF ** -0.5),
        "final_norm": gain(ks[19], (D_MODEL,)),
    }


def reference(x, positions, ffn1_norm, ffn1_w_in, ffn1_w_out, mix_norm, w_in, sgu_ln_g, sgu_ln_b,
              sgu_w, sgu_b, pool_w, pool_scale, proj_a, proj_b, proj_c, w_out, ffn2_norm,
              ffn2_w_in, ffn2_w_out, final_norm):
    B, S = x.shape[0], x.shape[1]
    cuts = np.cumsum(SPLITS)[:-1].tolist()
    h = x
    for l in range(DEPTH):
        h = h + 0.5 * swiglu(rms_norm(h, ffn1_norm[l]), ffn1_w_in[l], ffn1_w_out[l])
        n = rms_norm(h, mix_norm[l])
        q, k, v, iq, ik, iw, z, c, gates = jnp.split(n @ w_in[l], cuts, axis=-1)
        q = partial_rope(q.reshape(B, S, N_HEADS_A, HEAD_DIM), positions)
        k = partial_rope(k.reshape(B, S, N_KV_HEADS_A, HEAD_DIM), positions)
        v = v.reshape(B, S, N_KV_HEADS_A, HEAD_DIM)
        iq = partial_rope(iq.reshape(B, S, IDX_HEADS, IDX_DIM), positions)
        ik = partial_rope(ik.reshape(B, S, 1, IDX_DIM), positions)[:, :, 0]
        y_a = dsa_attention(q, k, v, iq, ik, iw)
        y_b = spatial_gating(z, sgu_ln_g[l], sgu_ln_b[l], sgu_w[l], sgu_b[l])
        y_c = multiscale_pool(c, pool_w[l], pool_scale[l])
        g_a, g_b, g_c = jnp.split(jax.nn.sigmoid(gates), N_BRANCHES, axis=-1)
        merged = g_a * (y_a @ proj_a[l]) + g_b * (y_b @ proj_b[l]) + g_c * (y_c @ proj_c[l])
        h = h + merged @ w_out[l]
        h = h + 0.5 * swiglu(rms_norm(h, ffn2_norm[l]), ffn2_w_in[l], ffn2_w_out[l])
    return rms_norm(h, final_norm)
```

```python
import math
from contextlib import ExitStack

import numpy as np
import concourse.bass as bass
import concourse.mybir as mybir
from concourse.bass_utils import run_bass_kernel_spmd

F32 = mybir.dt.float32
BF16 = mybir.dt.bfloat16
I32 = mybir.dt.int32
ALU = mybir.AluOpType
AF = mybir.ActivationFunctionType
AX = mybir.AxisListType
NCORE = 8
BIG = 1.0e30
EPS = 1e-6
ROPE_THETA = 500000.0
POOL_WINDOWS = (2, 4, 8, 16)

FULL_CFG = dict(D=4096, T=1024, F=11008, L=2, HQ=16, HKV=4, HI=16, SG=8, TOPK=256)


def derive(cfg):
    c = dict(cfg)
    c["KD"] = c["D"] // 128
    c["KF"] = c["F"] // 128
    c["NS"] = c["T"] // 128
    c["TN"] = min(512, c["T"])
    c["NTC"] = c["T"] // c["TN"]
    c["AW"] = c["HQ"] * 128
    c["KVW"] = c["HKV"] * 128
    c["IQW"] = c["HI"] * 64
    c["SW"] = c["SG"] * 128
    c["PW"] = 1024
    c["G"] = c["HQ"] // c["HKV"]
    offs = {}
    o = 0
    for name, wdt in (("q", c["AW"]), ("k", c["KVW"]), ("v", c["KVW"]), ("iq", c["IQW"]), ("ik", 64),
                      ("iw", c["HI"]), ("zu", c["SW"]), ("zv", c["SW"]), ("c", c["PW"]),
                      ("ga", c["D"]), ("gb", c["D"]), ("gc", c["D"])):
        offs[name] = o
        o += wdt
    c["OFF"] = offs
    c["NIN"] = o
    c["WSHAPES"] = {
        "ffn1_in": (c["D"], 2 * c["F"]), "ffn1_out": (c["F"], c["D"]), "w_in": (c["D"], c["NIN"]),
        "proj_a": (c["AW"], c["D"]), "proj_b": (c["SW"], c["D"]), "proj_c": (c["PW"], c["D"]),
        "w_out": (c["D"], c["D"]), "ffn2_in": (c["D"], 2 * c["F"]), "ffn2_out": (c["F"], c["D"]),
    }
    return c


WNAMES = ["ffn1_in", "ffn1_out", "w_in", "proj_a", "proj_b", "proj_c", "w_out", "ffn2_in", "ffn2_out"]


class Buf:
    __slots__ = ("name", "w", "r", "sem", "cnt")

    def __init__(self, name):
        self.name = name
        self.w = None
        self.r = {}
        self.sem = None
        self.cnt = 0


class Sched:
    def __init__(self, nc):
        self.nc = nc
        self.eng = {"pe": nc.tensor, "act": nc.scalar, "dve": nc.vector, "pool": nc.gpsimd, "sp": nc.sync}
        self.sem = {}
        self.cnt = {}
        for e in ("pe", "act", "dve", "pool"):
            self.sem[e] = nc.alloc_semaphore("prog_" + e)
            self.cnt[e] = 0
        self.ccsem = nc.alloc_semaphore("cc_sem")
        self.cccnt = 0
        self.waited = {e: {} for e in self.eng}
        self.allsems = {}
        self.bufs = {}
        self.nsem = 6
        self.nobar = {id(self.ccsem)}
        for e in ("pe", "act", "dve", "pool"):
            self.allsems[id(self.sem[e])] = [self.sem[e], 0]
        self.allsems[id(self.ccsem)] = [self.ccsem, 0]

    def buf(self, name):
        if name not in self.bufs:
            self.bufs[name] = Buf(name)
        return self.bufs[name]

    def _wait(self, eng, toks):
        wd = self.waited[eng]
        for sem, val in toks:
            k = id(sem)
            if wd.get(k, 0) < val:
                self.eng[eng].wait_ge(sem, val)
                wd[k] = val

    def _deps(self, r, w):
        toks = []
        for b in r:
            if b.w is not None:
                toks.append(b.w)
        for b in w:
            if b.w is not None:
                toks.append(b.w)
            toks.extend(b.r.values())
        return toks

    def _update(self, tok, r, w):
        k = id(tok[0])
        self.allsems[k][1] = max(self.allsems[k][1], tok[1])
        for b in r:
            b.r[k] = tok
        for b in w:
            b.w = tok
            b.r = {}

    def op(self, eng, fn, r=(), w=()):
        toks = self._deps(r, w)
        if eng == "pe":
            ps = self.sem["pe"]
            toks = [t for t in toks if t[0] is not ps]
        self._wait(eng, toks)
        self.cnt[eng] += 1
        tok = (self.sem[eng], self.cnt[eng])
        fn(self.eng[eng]).then_inc(tok[0], 1)
        self._update(tok, r, w)

    def dma(self, q, out, in_, r=(), w=(), anchor=None):
        if anchor.sem is None:
            anchor.sem = self.nc.alloc_semaphore("d_" + anchor.name)
            if anchor.name.startswith("relay"):
                self.nobar.add(id(anchor.sem))
            self.nsem += 1
            self.allsems[id(anchor.sem)] = [anchor.sem, 0]
        toks = self._deps(r, w)
        if anchor.cnt > 0:
            toks.append((anchor.sem, anchor.cnt))
        self._wait(q, toks)
        anchor.cnt += 16
        tok = (anchor.sem, anchor.cnt)
        self.eng[q].dma_start(out=out, in_=in_).then_inc(anchor.sem, 16)
        self._update(tok, r, w)

    def cc(self, kind, ins, outs, r=(), w=()):
        toks = self._deps(r, w)
        if self.cccnt > 0:
            toks.append((self.ccsem, self.cccnt))
        self._wait("pool", toks)
        self.cccnt += 1
        tok = (self.ccsem, self.cccnt)
        self.nc.gpsimd.collective_compute(kind, ALU.bypass, replica_groups=[list(range(NCORE))],
                                          ins=ins, outs=outs).then_inc(self.ccsem)
        self._update(tok, r, w)

    def barrier(self, engs=("pe", "act", "dve", "pool", "sp")):
        toks = [(s, v) for k_, (s, v) in self.allsems.items() if v > 0 and k_ not in self.nobar]
        for e in engs:
            self._wait(e, toks)


def dv(h, off, dims):
    return bass.AP(h, off, [list(d) for d in dims])


def build(cfg):
    c = derive(cfg)
    D, T, F, L = c["D"], c["T"], c["F"], c["L"]
    KD, KF, NS, TN, NTC = c["KD"], c["KF"], c["NS"], c["TN"], c["NTC"]
    HQ, HKV, HI, SG, G = c["HQ"], c["HKV"], c["HI"], c["SG"], c["G"]
    AW, KVW, IQW, SW, PW, NIN = c["AW"], c["KVW"], c["IQW"], c["SW"], c["PW"], c["NIN"]
    OFF = c["OFF"]
    IQC = IQW // 128
    TOPK = c["TOPK"]
    assert TOPK % 8 == 0 and G * 128 <= 512 and KD % 4 == 0

    nc = bass.Bass("TRN2", target_bir_lowering=False)
    S = Sched(nc)
    es = ExitStack()
    es.enter_context(nc.allow_low_precision("bf16 matmuls with fp32 accumulation"))
    es.enter_context(nc.allow_non_contiguous_dma("strided weight / activation tiles"))

    def din(name, shape, dt=F32):
        return nc.dram_tensor(name, list(shape), dt, kind="ExternalInput")

    def dint(name, shape, dt=BF16):
        return nc.dram_tensor(name, list(shape), dt, kind="Internal")

    xT = din("xT", [D, T])
    pos = din("pos", [1, T], I32)
    gains = din("gains", [128, (3 * L + 1) * KD])
    sgu_lng = din("sgu_lng", [L, SW])
    sgu_lnb = din("sgu_lnb", [L, SW])
    sgu_wT = din("sgu_wT", [L * 128, SG * 128])
    sgu_b = din("sgu_b", [L, SG * 128])
    pool_w = din("pool_w", [L * 128, 8 * 256])
    pool_scale = din("pool_scale", [L * 128, 8])
    tab_rope = din("tab_rope", [128, 4])
    tab_P = din("tab_P", [128, 256])
    tab_ident = din("tab_ident", [128, 128])
    tab_tril = din("tab_tril", [128, 128])
    tab_poolA = din("tab_poolA", [128, 2 * 4 * 128])
    tab_poolS = din("tab_poolS", [128, 2 * 4 * 2 * 128])
    tab_pen = din("tab_pen", [128, 1024])
    outT = nc.dram_tensor("outT", [D, T], F32, kind="ExternalOutput")

    wsrc, wsh, wfull, wM, wmo = {}, {}, {}, {}, {}
    WGROUPS = [["ffn1_in"], ["ffn2_in"], ["w_in", "ffn1_out"], ["ffn2_out", "proj_a", "proj_b", "proj_c", "w_out"]]
    wgrp, wsh_g, wg_g = {}, {}, {}
    for l in range(L):
        for gi, grp in enumerate(WGROUPS):
            mo_ = 0
            for n in grp:
                K_, N_ = c["WSHAPES"][n]
                wmo[(l, n)] = mo_
                wgrp[(l, n)] = (l, gi)
                mo_ += K_ * N_ // NCORE // 128
            wsh_g[(l, gi)] = dint(f"wsh_{l}_{gi}", [128, mo_])
            wg_g[(l, gi)] = dint(f"wg_{l}_{gi}", [NCORE * 128, mo_])
    for l in range(L):
        for n in WNAMES:
            K_, N_ = c["WSHAPES"][n]
            M = K_ * N_ // NCORE // 128
            assert K_ * N_ == M * NCORE * 128
            key = (l, n)
            wM[key] = M
            wsrc[key] = din(f"w{l}_{n}", [128, M])
            if len(WGROUPS[wgrp[key][1]]) == 1:
                wfull[key] = wg_g[wgrp[key]]
            else:
                wfull[key] = dint(f"g{l}_{n}", [NCORE * 128, M])

    hT = dint("hT", [D, T], F32)
    KA = max(KF, KD)
    aT = dint("aT", [KA * 128, T])
    nT = dint("nT", [D, T])
    qT = dint("qT", [AW, T])
    iqT = dint("iqT", [IQW, T])
    ybT = dint("ybT", [SW, T])
    ycT = dint("ycT", [PW, T])
    yaT = dint("yaT", [AW, T])
    zvd = dint("zvd", [T, SW], F32)
    cmine = dint("cmine", [T, PW])
    offK = 0
    offV = offK + KVW * T
    offI = offV + T * KVW
    offC = offI + 128 * T
    EXN = offC + NS * 16 * PW
    assert EXN % 128 == 0
    ex = dint("ex", [128, EXN // 128])
    exg = dint("exg", [NCORE * 128, EXN // 128])

    B = S.buf
    hb = [B(f"h{i}") for i in range(KD)]

    uniq = [0]

    def sb(name, shape, dt, stack=es):
        uniq[0] += 1
        return stack.enter_context(nc.sbuf_tensor(f"{name}_{uniq[0]}", list(shape), dt))

    gains_sb = sb("gains_sb", [128, (3 * L + 1) * KD], F32)
    ident_bf = sb("ident_bf", [128, 128], BF16)
    ones_bf = sb("ones_bf", [128, 128], BF16)
    ones_f = sb("ones_f", [128, 128], F32)
    P_sb = sb("P_sb", [128, 256], F32)
    rope_sb = sb("rope_sb", [128, 4], F32)
    pen_sb = sb("pen_sb", [128, 1024], F32)
    rstd = sb("rstd", [128, T], F32)
    iw_sb = sb("iw_sb", [128, NS * HI], F32)
    wsl = [sb(f"wsl{i}", [128, max(KD, AW // 128, SW // 128, PW // 128), 128], BF16) for i in range(4)]
    ps = [es.enter_context(nc.psum_tensor(f"ps{i}", [128, 512], F32)) for i in range(8)]
    psb = [B(f"ps{i}") for i in range(8)]
    XNb = B("XN")
    wslb = [B(f"wsl{i}") for i in range(4)]
    wctr = [0]

    def wslot():
        i = wctr[0] % 4
        wctr[0] += 1
        return wsl[i], wslb[i]

    def load(sb_ap, dr_ap, sbuf_b, dram_b, q="sp"):
        S.dma(q, sb_ap, dr_ap, r=[dram_b], w=[sbuf_b], anchor=sbuf_b)

    def store(dr_ap, sb_ap, dram_b, sbuf_b, q="act"):
        S.dma(q, dr_ap, sb_ap, r=[sbuf_b], w=[dram_b], anchor=sbuf_b)

    def WB(key):
        return B(f"gw_{key[0]}_{key[1]}")

    def wview(key, k0, nk, n0, nn):
        N_ = c["WSHAPES"][key[1]][1]
        return dv(wfull[key], k0 * N_ + n0, [[N_, 128], [128 * N_, nk], [1, nn]])

    def hview(t, ci):
        return dv(t, ci * 128 * T, [[T, 128], [1, T]])

    cb = B("consts")
    with ExitStack() as p0:
        stg = sb("c_stg", [128, 128], F32, p0)
        stgb = B("c_stg")
        load(gains_sb[:], gains.ap(), cb, B("in_gains"))
        load(P_sb[:], tab_P.ap(), B("P_sb"), B("in_P"))
        load(rope_sb[:], tab_rope.ap(), B("rope_sb"), B("in_rope"))
        load(pen_sb[:], tab_pen.ap(), B("pen_sb"), B("in_pen"))
        load(stg[:], tab_ident.ap(), stgb, B("in_ident"))
        S.op("dve", lambda e: e.tensor_copy(out=ident_bf[:], in_=stg[:]), r=[stgb], w=[B("ident")])
        S.op("dve", lambda e: e.memset(ones_bf[:], 1.0), w=[B("ones_bf")])
        S.op("dve", lambda e: e.memset(ones_f[:], 1.0), w=[B("ones_f")])
        for ci in range(KD):
            S.dma("sp", hview(hT, ci), hview(xT, ci), r=[B("in_x")], w=[hb[ci]], anchor=B("hcopy"))
        PC = 2048
        cin = [sb(f"cin{i}", [128, PC], F32, p0) for i in range(3)]
        cout = [sb(f"cout{i}", [128, PC], BF16, p0) for i in range(3)]
        cinb = [B(f"cin{i}") for i in range(3)]
        coutb = [B(f"cout{i}") for i in range(3)]
        k = 0
        for l in range(L):
            for n in WNAMES:
                key = (l, n)
                M = wM[key]
                shb = B(f"s{wgrp[key]}")
                for m0 in range(0, M, PC):
                    mm = min(PC, M - m0)
                    s = k % 3
                    load(cin[s][:, 0:mm], wsrc[key].ap()[:, m0:m0 + mm], cinb[s], B("in_w"))
                    if k % 2 == 0:
                        S.op("dve", lambda e: e.tensor_copy(out=cout[s][:, 0:mm], in_=cin[s][:, 0:mm]), r=[cinb[s]], w=[coutb[s]])
                    else:
                        S.op("act", lambda e: e.activation(out=cout[s][:, 0:mm], in_=cin[s][:, 0:mm], func=AF.Copy),
                             r=[cinb[s]], w=[coutb[s]])
                    store(wsh_g[wgrp[key]].ap()[:, wmo[key] + m0:wmo[key] + m0 + mm], cout[s][:, 0:mm], shb, coutb[s])
                    k += 1
        for (l, gi) in [(l_, g_) for l_ in range(L) for g_ in (0, 2, 1, 3)]:
            grp = WGROUPS[gi]
            single = (len(grp) == 1)
            S.cc("AllGather", [wsh_g[(l, gi)].ap()], [wg_g[(l, gi)].ap()], r=[B(f"s{(l, gi)}")],
                 w=[B(f"wg{(l, gi)}")] + ([WB((l, grp[0]))] if single else []))
            if not single:
                for n in grp:
                    key = (l, n)
                    M = wM[key]
                    for rr_ in range(NCORE):
                        S.dma("pool", wfull[key].ap()[rr_ * 128:(rr_ + 1) * 128, :],
                              wg_g[(l, gi)].ap()[rr_ * 128:(rr_ + 1) * 128, wmo[key]:wmo[key] + M],
                              r=[B(f"wg{(l, gi)}")], w=[WB(key)], anchor=B(f"relay{rr_ % 4}"))
        S.barrier()

    def emit_rope_tables(cs_tab):
        with ExitStack() as p0:
            posi = sb("posi", [128, T], I32, p0)
            posf = sb("posf", [128, T], F32, p0)
            ang = sb("ang", [128, T], F32, p0)
            tq = sb("tq", [128, T], F32, p0)
            ti = sb("ti", [128, T], I32, p0)
            tf = sb("tf", [128, T], F32, p0)
            rr = sb("rr", [128, T], F32, p0)
            posib, posfb, angb, tqb, tib, tfb, rrb = (B(n) for n in ("posi", "posf", "ang", "tq", "ti", "tf", "rr"))
            csb = B("cs_tab")
            load(posi[:], pos.ap()[0:1, :].to_broadcast([128, T]), posib, B("in_pos"))
            S.op("dve", lambda e: e.tensor_copy(out=posf[:], in_=posi[:]), r=[posib], w=[posfb])
            for kind in range(2):
                fcol = rope_sb[:, 2 * kind:2 * kind + 1]
                scol = rope_sb[:, 2 * kind + 1:2 * kind + 2]
                S.op("dve", lambda e: e.tensor_scalar(out=ang[:], in0=posf[:], scalar1=fcol, scalar2=None, op0=ALU.mult),
                     r=[posfb, B("rope_sb")], w=[angb])
                for which in range(2):
                    shift = math.pi / 2 if which == 0 else 0.0
                    S.op("dve", lambda e: e.tensor_scalar(out=tq[:], in0=ang[:], scalar1=shift, scalar2=1.0 / (2 * math.pi),
                                                          op0=ALU.add, op1=ALU.mult), r=[angb], w=[tqb])
                    S.op("dve", lambda e: e.tensor_copy(out=ti[:], in_=tq[:]), r=[tqb], w=[tib])
                    S.op("dve", lambda e: e.tensor_copy(out=tf[:], in_=ti[:]), r=[tib], w=[tfb])
                    S.op("dve", lambda e: e.scalar_tensor_tensor(out=rr[:], in0=tf[:], scalar=-2 * math.pi, in1=ang[:],
                                                                 op0=ALU.mult, op1=ALU.add), r=[tfb, angb], w=[rrb])
                    if which == 0:
                        S.op("act", lambda e: e.activation(out=cs_tab[:, 2 * kind, :], in_=rr[:], func=AF.Sin, bias=0.0, scale=1.0),
                             r=[rrb], w=[csb]) if False else None
                        S.op("dve", lambda e: e.tensor_scalar(out=rr[:], in0=rr[:], scalar1=shift, scalar2=None, op0=ALU.add),
                             r=[rrb], w=[rrb])
                        S.op("dve", lambda e: e.tensor_scalar(out=tq[:], in0=rr[:], scalar1=math.pi, scalar2=-2 * math.pi,
                                                              op0=ALU.is_gt, op1=ALU.mult), r=[rrb], w=[tqb])
                        S.op("dve", lambda e: e.tensor_tensor(out=rr[:], in0=rr[:], in1=tq[:], op=ALU.add), r=[rrb, tqb], w=[rrb])
                        S.op("act", lambda e: e.activation(out=cs_tab[:, 2 * kind, :], in_=rr[:], func=AF.Sin),
                             r=[rrb], w=[csb])
                    else:
                        S.op("act", lambda e: e.activation(out=rr[:], in_=rr[:], func=AF.Sin), r=[rrb], w=[rrb])
                        S.op("dve", lambda e: e.tensor_scalar(out=cs_tab[:, 2 * kind + 1, :], in0=rr[:], scalar1=scol,
                                                              scalar2=None, op0=ALU.mult), r=[rrb, B("rope_sb")], w=[csb])

            S.barrier()

    def emit_norm(gidx, stack, XN, spill=None):
        hst = [sb(f"n_hst{i}", [128, T], F32, stack) for i in range(2)]
        sq = [sb(f"n_sq{i}", [128, T], F32, stack) for i in range(2)]
        tmp = sb("n_tmp", [128, T], F32, stack)
        hstb = [B("n_hst0"), B("n_hst1")]
        sqb = [B("n_sq0"), B("n_sq1")]
        tmpb, rstdb = B("n_tmp"), B("rstd")
        for ci in range(KD):
            s = ci % 2
            load(hst[s][:], hview(hT, ci), hstb[s], hb[ci])
            S.op("act", lambda e: e.activation(out=sq[s][:], in_=hst[s][:], func=AF.Square), r=[hstb[s]], w=[sqb[s]])
            for tc in range(NTC):
                S.op("pe", lambda e: e.matmul(ps[tc][:, 0:TN], lhsT=ones_f[:], rhs=sq[s][:, tc * TN:(tc + 1) * TN],
                                              start=(ci == 0), stop=(ci == KD - 1)), r=[sqb[s], B("ones_f")], w=[psb[tc]])
        for tc in range(NTC):
            S.op("act", lambda e: e.activation(out=tmp[:, tc * TN:(tc + 1) * TN], in_=ps[tc][:, 0:TN], func=AF.Sqrt,
                                               bias=EPS, scale=1.0 / D), r=[psb[tc]], w=[tmpb])
        S.op("dve", lambda e: e.reciprocal(out=rstd[:], in_=tmp[:]), r=[tmpb], w=[rstdb])
        for ci in range(KD):
            s = ci % 2
            load(hst[s][:], hview(hT, ci), hstb[s], hb[ci])
            gcol = gains_sb[:, gidx * KD + ci:gidx * KD + ci + 1]
            S.op("dve", lambda e: e.scalar_tensor_tensor(out=XN[:, ci, :], in0=hst[s][:], scalar=gcol, in1=rstd[:],
                                                         op0=ALU.mult, op1=ALU.mult), r=[hstb[s], rstdb, cb], w=[XNb])
        if spill is not None:
            for ci in range(KD):
                store(hview(spill, ci), XN[:, ci, :], B("nT"), XNb)

    def emit_outproj(A, nk, wkey, scale, stack, abuf):
        at = [sb(f"o_at{i}", [128, T], BF16, stack) for i in range(3)]
        w2 = [sb(f"o_w2{i}", [128, 512], BF16, stack) for i in range(3)]
        hst = [sb(f"o_hst{i}", [128, T], F32, stack) for i in range(2)]
        hnw = [sb(f"o_hnw{i}", [128, T], F32, stack) for i in range(2)]
        atb = [B(f"o_at{i}") for i in range(3)]
        w2b = [B(f"o_w2{i}") for i in range(3)]
        hstb = [B(f"o_hst{i}") for i in range(2)]
        hnwb = [B(f"o_hnw{i}") for i in range(2)]
        it = 0
        ev = 0
        for dg in range(KD // 4):
            for k in range(nk):
                s = it % 3
                it += 1
                load(at[s][:], hview(A, k), atb[s], abuf)
                load(w2[s][:], wview(wkey, k * 128, 1, dg * 512, 512)[:, 0, :], w2b[s], WB(wkey))
                for dc in range(4):
                    for tc in range(NTC):
                        S.op("pe", lambda e: e.matmul(ps[dc * NTC + tc][:, 0:TN], lhsT=w2[s][:, dc * 128:(dc + 1) * 128],
                                                      rhs=at[s][:, tc * TN:(tc + 1) * TN], start=(k == 0), stop=(k == nk - 1)),
                             r=[atb[s], w2b[s]], w=[psb[dc * NTC + tc]])
            for dc in range(4):
                ci = dg * 4 + dc
                s = ev % 2
                ev += 1
                load(hst[s][:], hview(hT, ci), hstb[s], hb[ci])
                for tc in range(NTC):
                    S.op("dve", lambda e: e.scalar_tensor_tensor(out=hnw[s][:, tc * TN:(tc + 1) * TN], in0=ps[dc * NTC + tc][:, 0:TN],
                                                                 scalar=scale, in1=hst[s][:, tc * TN:(tc + 1) * TN],
                                                                 op0=ALU.mult, op1=ALU.add),
                         r=[psb[dc * NTC + tc], hstb[s]], w=[hnwb[s]])
                store(hview(hT, ci), hnw[s][:], hb[ci], hnwb[s])

    def emit_ffn(l, which):
        gidx = 3 * l + (0 if which == 1 else 2)
        kin, kout = (l, f"ffn{which}_in"), (l, f"ffn{which}_out")
        aTb = B("aT")
        with ExitStack() as st:
            XN = sb("XN", [128, KD, T], BF16, st)
            emit_norm(gidx, st, XN)
            sg = [sb(f"f_sg{i}", [128, TN], F32, st) for i in range(2)]
            a_t = [sb(f"f_at{i}", [128, T], BF16, st) for i in range(2)]
            sgb = [B("f_sg0"), B("f_sg1")]
            a_tb = [B("f_at0"), B("f_at1")]
            n = 0
            for i in range(KF):
                wg, wgb = wslot()
                load(wg[:, 0:KD, :], wview(kin, 0, KD, i * 128, 128), wgb, WB(kin))
                wu, wub = wslot()
                load(wu[:, 0:KD, :], wview(kin, 0, KD, F + i * 128, 128), wub, WB(kin))
                par = (i % 2) * 4
                for tc in range(NTC):
                    for kc in range(KD):
                        S.op("pe", lambda e: e.matmul(ps[par + tc][:, 0:TN], lhsT=wg[:, kc, :], rhs=XN[:, kc, tc * TN:(tc + 1) * TN],
                                                      start=(kc == 0), stop=(kc == KD - 1)), r=[wgb, XNb], w=[psb[par + tc]])
                    for kc in range(KD):
                        S.op("pe", lambda e: e.matmul(ps[par + 2 + tc][:, 0:TN], lhsT=wu[:, kc, :], rhs=XN[:, kc, tc * TN:(tc + 1) * TN],
                                                      start=(kc == 0), stop=(kc == KD - 1)), r=[wub, XNb], w=[psb[par + 2 + tc]])
                sa = i % 2
                for tc in range(NTC):
                    s2 = n % 2
                    n += 1
                    S.op("act", lambda e: e.activation(out=sg[s2][:], in_=ps[par + tc][:, 0:TN], func=AF.Silu),
                         r=[psb[par + tc]], w=[sgb[s2]])
                    S.op("dve", lambda e: e.tensor_tensor(out=a_t[sa][:, tc * TN:(tc + 1) * TN], in0=sg[s2][:],
                                                          in1=ps[par + 2 + tc][:, 0:TN], op=ALU.mult),
                         r=[sgb[s2], psb[par + 2 + tc]], w=[a_tb[sa]])
                store(hview(aT, i), a_t[sa][:], aTb, a_tb[sa])
            S.barrier()
        with ExitStack() as st:
            emit_outproj(aT, KF, kout, 0.5, st, aTb)
            S.barrier()

    def emit_gelu(out_ap, x_ap, t1_ap, t2_ap, xb, t1b, t2b, outb):
        S.op("dve", lambda e: e.tensor_tensor(out=t1_ap, in0=x_ap, in1=x_ap, op=ALU.mult), r=[xb], w=[t1b])
        S.op("dve", lambda e: e.tensor_scalar(out=t1_ap, in0=t1_ap, scalar1=0.044715, scalar2=1.0, op0=ALU.mult, op1=ALU.add),
             r=[t1b], w=[t1b])
        S.op("dve", lambda e: e.tensor_tensor(out=t1_ap, in0=t1_ap, in1=x_ap, op=ALU.mult), r=[t1b, xb], w=[t1b])
        S.op("act", lambda e: e.activation(out=t2_ap, in_=t1_ap, func=AF.Sigmoid, scale=1.5957691216057308), r=[t1b], w=[t2b])
        S.op("dve", lambda e: e.tensor_tensor(out=out_ap, in0=x_ap, in1=t2_ap, op=ALU.mult), r=[xb, t2b], w=[outb])

    def emit_mixer(l):
        kw = (l, "w_in")
        gwb = B("gw")
        outer = ExitStack()
        XN = sb("XN", [128, KD, T], BF16, outer)
        cs_tab = sb("cs_tab", [128, 4, T], F32, outer)
        emit_rope_tables(cs_tab)
        with ExitStack() as st:
            emit_norm(3 * l + 1, st, XN, spill=nT)
            xs = [sb(f"m_xs{i}", [128, TN], F32, st) for i in range(2)]
            t1 = [sb(f"m_t1{i}", [128, TN], F32, st) for i in range(2)]
            t2 = [sb(f"m_t2{i}", [128, TN], F32, st) for i in range(2)]
            ob = [sb(f"m_ob{i}", [128, T], BF16, st) for i in range(2)]
            xsb = [B("m_xs0"), B("m_xs1")]
            t1b = [B("m_t10"), B("m_t11")]
            t2b = [B("m_t20"), B("m_t21")]
            obb = [B("m_ob0"), B("m_ob1")]
            csb = B("cs_tab")
            Pb = B("P_sb")
            chunks = [("q", h, OFF["q"] + h * 128, 0) for h in range(HQ)] + \
                     [("k", j, OFF["k"] + j * 128, 0) for j in range(HKV)] + \
                     [("iq", ci, OFF["iq"] + ci * 128, 1) for ci in range(IQC)] + [("ik", 0, OFF["ik"], 1)]
            n = 0
            for ix, (kind, idx, col, rk) in enumerate(chunks):
                w, wb = wslot()
                if kind == "ik":
                    load(w[:, 0:KD, 0:64], wview(kw, 0, KD, col, 64), wb, WB(kw))
                    load(w[:, 0:KD, 64:128], wview(kw, 0, KD, col, 64), wb, WB(kw))
                else:
                    load(w[:, 0:KD, :], wview(kw, 0, KD, col, 128), wb, WB(kw))
                par = (ix % 2) * 4
                so = ix % 2
                for tc in range(NTC):
                    for kc in range(KD):
                        S.op("pe", lambda e: e.matmul(ps[par + tc][:, 0:TN], lhsT=w[:, kc, :], rhs=XN[:, kc, tc * TN:(tc + 1) * TN],
                                                      start=(kc == 0), stop=(kc == KD - 1)), r=[wb, XNb], w=[psb[par + tc]])
                for tc in range(NTC):
                    s2 = n % 2
                    n += 1
                    tsl = slice(tc * TN, (tc + 1) * TN)
                    S.op("act", lambda e: e.activation(out=xs[s2][:], in_=ps[par + tc][:, 0:TN], func=AF.Copy),
                         r=[psb[par + tc]], w=[xsb[s2]])
                    S.op("pe", lambda e: e.matmul(ps[par + 2 + tc][:, 0:TN], lhsT=P_sb[:, rk * 128:(rk + 1) * 128], rhs=xs[s2][:],
                                                  start=True, stop=True), r=[xsb[s2], Pb], w=[psb[par + 2 + tc]])
                    S.op("dve", lambda e: e.tensor_tensor(out=t1[s2][:], in0=xs[s2][:], in1=cs_tab[:, 2 * rk, tsl], op=ALU.mult),
                         r=[xsb[s2], csb], w=[t1b[s2]])
                    S.op("dve", lambda e: e.tensor_tensor(out=t2[s2][:], in0=ps[par + 2 + tc][:, 0:TN], in1=cs_tab[:, 2 * rk + 1, tsl],
                                                          op=ALU.mult), r=[psb[par + 2 + tc], csb], w=[t2b[s2]])
                    S.op("dve", lambda e: e.tensor_tensor(out=ob[so][:, tsl], in0=t1[s2][:], in1=t2[s2][:], op=ALU.add),
                         r=[t1b[s2], t2b[s2]], w=[obb[so]])
                if kind == "q":
                    store(hview(qT, idx), ob[so][:], B("qT"), obb[so])
                elif kind == "k":
                    store(dv(ex, offK + idx * 128 * T, [[T, 128], [1, T]]), ob[so][:], B("ex"), obb[so])
                elif kind == "iq":
                    store(hview(iqT, idx), ob[so][:], B("iqT"), obb[so])
                else:
                    store(dv(ex, offI, [[T, 128], [1, T]]), ob[so][:], B("ex"), obb[so])

            stv = [sb(f"m_stb{i}", [128, NS, 128], BF16, st) for i in range(2)]
            stf = [sb(f"m_stf{i}", [128, NS, 128], F32, st) for i in range(2)]
            stvb = [B("m_stb0"), B("m_stb1")]
            stfb = [B("m_stf0"), B("m_stf1")]
            pieces = [("v", p, OFF["v"] + p * 128, 128) for p in range(KVW // 128)] + [("iw", 0, OFF["iw"], HI)] + \
                     [("zv", p, OFF["zv"] + p * 128, 128) for p in range(SW // 128)] + \
                     [("c", p, OFF["c"] + p * 128, 128) for p in range(PW // 128)]
            bank = 0
            for ix, (kind, pidx, col, ncol) in enumerate(pieces):
                w, wb = wslot()
                load(w[:, 0:KD, 0:ncol], wview(kw, 0, KD, col, ncol), wb, WB(kw))
                s2 = ix % 2
                for i in range(NS):
                    bk = bank % 8
                    bank += 1
                    for kc in range(KD):
                        S.op("pe", lambda e: e.matmul(ps[bk][:, 0:ncol], lhsT=XN[:, kc, i * 128:(i + 1) * 128], rhs=w[:, kc, 0:ncol],
                                                      start=(kc == 0), stop=(kc == KD - 1)), r=[wb, XNb], w=[psb[bk]])
                    if kind == "iw":
                        S.op("act", lambda e: e.activation(out=iw_sb[:, i * HI:(i + 1) * HI], in_=ps[bk][:, 0:ncol], func=AF.Copy),
                             r=[psb[bk]], w=[B("iw_sb")])
                    elif kind == "zv":
                        S.op("act", lambda e: e.activation(out=stf[s2][:, i, :], in_=ps[bk][:, 0:128], func=AF.Copy),
                             r=[psb[bk]], w=[stfb[s2]])
                    else:
                        S.op("dve", lambda e: e.tensor_copy(out=stv[s2][:, i, :], in_=ps[bk][:, 0:128]),
                             r=[psb[bk]], w=[stvb[s2]])
                if kind == "v":
                    store(dv(ex, offV + pidx * 128, [[KVW, 128], [128 * KVW, NS], [1, 128]]), stv[s2][:], B("ex"), stvb[s2])
                elif kind == "zv":
                    store(dv(zvd, pidx * 128, [[SW, 128], [128 * SW, NS], [1, 128]]), stf[s2][:], B("zvd"), stfb[s2])
                elif kind == "c":
                    store(dv(cmine, pidx * 128, [[PW, 128], [128 * PW, NS], [1, 128]]), stv[s2][:], B("cmine"), stvb[s2])
            S.barrier()

        with ExitStack() as st:
            lng = sb("g_lng", [128, SW], F32, st)
            lnb = sb("g_lnb", [128, SW], F32, st)
            wsf = sb("g_wsf", [128, SG, 128], F32, st)
            tril = sb("g_tril", [128, 128], F32, st)
            wsb_ = sb("g_wsb", [128, SG, 128], BF16, st)
            bf_ = sb("g_bf", [1, SG * 128], F32, st)
            bb_ = sb("g_bb", [1, SG * 128], BF16, st)
            sT = sb("g_sT", [128, SG, T], BF16, st)
            zv = [sb(f"g_zv{i}", [128, SW], F32, st) for i in range(2)]
            ta = sb("g_ta", [128, SW], F32, st)
            tb = sb("g_tb", [128, SW], F32, st)
            gv = sb("g_gv", [128, SW], F32, st)
            vln = sb("g_vln", [128, SW], BF16, st)
            st1 = sb("g_st1", [128, 4], F32, st)
            lngb, lnbb, wsfb, trilb, wsbb, bfb, bbb, sTb = (B(n) for n in
                                                          ("g_lng", "g_lnb", "g_wsf", "g_tril", "g_wsb", "g_bf", "g_bb", "g_sT"))
            zvb = [B("g_zv0"), B("g_zv1")]
            tab_, tbb_, gvb, vlnb, st1b = B("g_ta"), B("g_tb"), B("g_gv"), B("g_vln"), B("g_st1")
            load(lng[:], sgu_lng.ap()[l:l + 1, :].to_broadcast([128, SW]), lngb, B("in_s"))
            load(lnb[:], sgu_lnb.ap()[l:l + 1, :].to_broadcast([128, SW]), lnbb, B("in_s"))
            load(wsf[:], sgu_wT.ap()[l * 128:(l + 1) * 128, :].rearrange("p (g i) -> p g i", g=SG), wsfb, B("in_s"))
            load(tril[:], tab_tril.ap(), trilb, B("in_s"))
            load(bf_[:], sgu_b.ap()[l:l + 1, :], bfb, B("in_s"))
            S.op("dve", lambda e: e.tensor_tensor(out=wsb_[:], in0=wsf[:], in1=tril[:].unsqueeze(1).to_broadcast([128, SG, 128]),
                                                  op=ALU.mult), r=[wsfb, trilb], w=[wsbb])
            S.op("dve", lambda e: e.tensor_copy(out=bb_[:], in_=bf_[:]), r=[bfb], w=[bbb])
            bank = 0
            for i in range(NS):
                s = i % 2
                load(zv[s][:], dv(zvd, i * 128 * SW, [[SW, 128], [1, SW]]), zvb[s], B("zvd"))
                emit_gelu(gv[:], zv[s][:], ta[:], tb[:], zvb[s], tab_, tbb_, gvb)
                S.op("dve", lambda e: e.reduce_sum(out=st1[:, 0:1], in_=gv[:], axis=AX.X), r=[gvb], w=[st1b])
                S.op("dve", lambda e: e.tensor_scalar(out=st1[:, 1:2], in0=st1[:, 0:1], scalar1=-1.0 / SW, scalar2=None, op0=ALU.mult),
                     r=[st1b], w=[st1b])
                S.op("dve", lambda e: e.tensor_scalar(out=ta[:], in0=gv[:], scalar1=st1[:, 1:2], scalar2=None, op0=ALU.add),
                     r=[gvb, st1b], w=[tab_])
                S.op("dve", lambda e: e.tensor_tensor(out=tb[:], in0=ta[:], in1=ta[:], op=ALU.mult), r=[tab_], w=[tbb_])
                S.op("dve", lambda e: e.reduce_sum(out=st1[:, 2:3], in_=tb[:], axis=AX.X), r=[tbb_], w=[st1b])
                S.op("act", lambda e: e.activation(out=st1[:, 3:4], in_=st1[:, 2:3], func=AF.Sqrt, bias=EPS, scale=1.0 / SW),
                     r=[st1b], w=[st1b])
                S.op("dve", lambda e: e.reciprocal(out=st1[:, 2:3], in_=st1[:, 3:4]), r=[st1b], w=[st1b])
                S.op("dve", lambda e: e.scalar_tensor_tensor(out=tb[:], in0=ta[:], scalar=st1[:, 2:3], in1=lng[:],
                                                             op0=ALU.mult, op1=ALU.mult), r=[tab_, st1b, lngb], w=[tbb_])
                S.op("dve", lambda e: e.tensor_tensor(out=vln[:], in0=tb[:], in1=lnb[:], op=ALU.add), r=[tbb_, lnbb], w=[vlnb])
                for g in range(SG):
                    bk = bank % 8
                    bank += 1
                    S.op("pe", lambda e: e.matmul(ps[bk][:, 0:128], lhsT=vln[:, g * 128:(g + 1) * 128], rhs=wsb_[:, g, :],
                                                  start=True, stop=False), r=[vlnb, wsbb], w=[psb[bk]])
                    S.op("pe", lambda e: e.matmul(ps[bk][:, 0:128], lhsT=ones_bf[0:1, :], rhs=bb_[0:1, g * 128:(g + 1) * 128],
                                                  start=False, stop=True), r=[bbb, B("ones_bf")], w=[psb[bk]])
                    S.op("act", lambda e: e.activation(out=sT[:, g, i * 128:(i + 1) * 128], in_=ps[bk][:, 0:128], func=AF.Copy),
                         r=[psb[bk]], w=[sTb])
            xs = [sb(f"g_xs{i}", [128, TN], F32, st) for i in range(2)]
            u1 = sb("g_u1", [128, TN], F32, st)
            u2 = sb("g_u2", [128, TN], F32, st)
            u3 = sb("g_u3", [128, TN], F32, st)
            ob = [sb(f"g_ob{i}", [128, T], BF16, st) for i in range(2)]
            xsb = [B("g_xs0"), B("g_xs1")]
            u1b, u2b, u3b = B("g_u1"), B("g_u2"), B("g_u3")
            obb = [B("g_ob0"), B("g_ob1")]
            n = 0
            for g in range(SG):
                w, wb = wslot()
                load(w[:, 0:KD, :], wview(kw, 0, KD, OFF["zu"] + g * 128, 128), wb, WB(kw))
                par = (g % 2) * 4
                so = g % 2
                for tc in range(NTC):
                    for kc in range(KD):
                        S.op("pe", lambda e: e.matmul(ps[par + tc][:, 0:TN], lhsT=w[:, kc, :], rhs=XN[:, kc, tc * TN:(tc + 1) * TN],
                                                      start=(kc == 0), stop=(kc == KD - 1)), r=[wb, XNb], w=[psb[par + tc]])
                for tc in range(NTC):
                    s2 = n % 2
                    n += 1
                    tsl = slice(tc * TN, (tc + 1) * TN)
                    S.op("act", lambda e: e.activation(out=xs[s2][:], in_=ps[par + tc][:, 0:TN], func=AF.Copy),
                         r=[psb[par + tc]], w=[xsb[s2]])
                    emit_gelu(u3[:], xs[s2][:], u1[:], u2[:], xsb[s2], u1b, u2b, u3b)
                    S.op("dve", lambda e: e.tensor_tensor(out=ob[so][:, tsl], in0=u3[:], in1=sT[:, g, tsl], op=ALU.mult),
                         r=[u3b, sTb], w=[obb[so]])
                store(hview(ybT, g), ob[so][:], B("ybT"), obb[so])
            S.barrier()

        outer.close()
        S.dma("sp", dv(ex, offC, [[16 * PW, NS], [PW, 16], [1, PW]]), dv(cmine, 112 * PW, [[128 * PW, NS], [PW, 16], [1, PW]]),
              r=[B("cmine")], w=[B("ex")], anchor=B("ctx"))
        S.cc("AllGather", [ex.ap()], [exg.ap()], r=[B("ex")], w=[B("exg")])
        S.barrier()

        WMAX = NS * 1024
        with ExitStack() as st:
            sc = sb("a_sc", [128, WMAX], F32, st)
            wk = sb("a_wk", [128, WMAX], F32, st)
            mk = sb("a_mk", [128, WMAX], BF16, st)
            mkT = sb("a_mkT", [128, WMAX], BF16, st)
            m8 = sb("a_m8", [128, 8], F32, st)
            thr = sb("a_thr", [128, 1], F32, st)
            iq_sb = sb("a_iq", [128, IQC, 128], BF16, st)
            ik_sb = [sb(f"a_ik{i}", [128, 8, 128], BF16, st) for i in range(2)]
            r_sb = [sb(f"a_r{i}", [128, 512], F32, st) for i in range(2)]
            q_sb = [sb(f"a_q{i}", [128, G, 128], BF16, st) for i in range(2)]
            k_sb = [sb(f"a_k{i}", [128, 8, 128], BF16, st) for i in range(2)]
            v_sb = [sb(f"a_v{i}", [128, 8, 128], BF16, st) for i in range(2)]
            p_sb = [sb(f"a_p{i}", [128, G, 128], BF16, st) for i in range(2)]
            pm_sb = [sb(f"a_pm{i}", [128, G, 128], BF16, st) for i in range(2)]
            rd_sb = sb("a_rd", [128, G * 128], F32, st)
            ya_sb = [sb(f"a_ya{i}", [128, G, 128], BF16, st) for i in range(2)]
            scb, wkb, mkb, mkTb, m8b, thrb, iqb, rdb = (B(n) for n in ("a_sc", "a_wk", "a_mk", "a_mkT", "a_m8", "a_thr", "a_iq", "a_rd"))
            ikb = [B("a_ik0"), B("a_ik1")]
            rb = [B("a_r0"), B("a_r1")]
            qb = [B("a_q0"), B("a_q1")]
            kb_ = [B("a_k0"), B("a_k1")]
            vb_ = [B("a_v0"), B("a_v1")]
            pb = [B("a_p0"), B("a_p1")]
            pmb = [B("a_pm0"), B("a_pm1")]
            yab = [B("a_ya0"), B("a_ya1")]
            GW = G * 128
            nik = nr = nq = nkv = npp = nbank = 0
            for i in range(NS):
                W = (i + 1) * 1024
                NKG = i + 1
                load(iq_sb[:], dv(iqT, i * 128, [[T, 128], [128 * T, IQC], [1, 128]]), iqb, B("iqT"))
                for kgi in range(NKG):
                    s = nik % 2
                    nik += 1
                    load(ik_sb[s][:], dv(exg, offI + kgi * 128, [[T, 128], [EXN, 8], [1, 128]]), ikb[s], B("exg"))
                    for hf in range(2):
                        cols = slice(kgi * 1024 + hf * 512, kgi * 1024 + (hf + 1) * 512)
                        for h in range(HI):
                            pbase = (h % 2) * 64
                            bk = nbank % 4
                            nbank += 1
                            s2 = nr % 2
                            nr += 1
                            S.op("pe", lambda e: e.matmul(ps[bk][:, :], lhsT=iq_sb[pbase:pbase + 64, h // 2, :],
                                                          rhs=ik_sb[s][pbase:pbase + 64, 4 * hf:4 * hf + 4, :], start=True, stop=True),
                                 r=[iqb, ikb[s]], w=[psb[bk]])
                            S.op("act", lambda e: e.activation(out=r_sb[s2][:], in_=ps[bk][:, :], func=AF.Relu),
                                 r=[psb[bk]], w=[rb[s2]])
                            wcol = iw_sb[:, i * HI + h:i * HI + h + 1]
                            if h == 0:
                                S.op("dve", lambda e: e.tensor_scalar(out=sc[:, cols], in0=r_sb[s2][:], scalar1=wcol, scalar2=None,
                                                                      op0=ALU.mult), r=[rb[s2], B("iw_sb")], w=[scb])
                            else:
                                S.op("dve", lambda e: e.scalar_tensor_tensor(out=sc[:, cols], in0=r_sb[s2][:], scalar=wcol, in1=sc[:, cols],
                                                                             op0=ALU.mult, op1=ALU.add), r=[rb[s2], B("iw_sb"), scb], w=[scb])
                S.op("dve", lambda e: e.tensor_tensor(out=sc[:, W - 1024:W], in0=sc[:, W - 1024:W], in1=pen_sb[:], op=ALU.add),
                     r=[scb, B("pen_sb")], w=[scb])
                S.op("dve", lambda e: e.tensor_copy(out=wk[:, 0:W], in_=sc[:, 0:W]), r=[scb], w=[wkb])
                for rnd in range(TOPK // 8):
                    S.op("dve", lambda e: e.max(out=m8[:], in_=wk[:, 0:W]), r=[wkb], w=[m8b])
                    if rnd < TOPK // 8 - 1:
                        S.op("dve", lambda e: e.match_replace(out=wk[:, 0:W], in_to_replace=m8[:], in_values=wk[:, 0:W], imm_value=-BIG),
                             r=[wkb, m8b], w=[wkb])
                S.op("dve", lambda e: e.tensor_scalar(out=thr[:], in0=m8[:, 7:8], scalar1=-BIG / 2, scalar2=None, op0=ALU.max),
                     r=[m8b], w=[thrb])
                S.op("dve", lambda e: e.tensor_scalar(out=mk[:, 0:W], in0=sc[:, 0:W], scalar1=thr[:, 0:1], scalar2=None, op0=ALU.is_ge),
                     r=[scb, thrb], w=[mkb])
                for kb4 in range(W // 512):
                    bk = 4 + (kb4 % 2)
                    for j4 in range(4):
                        kb = kb4 * 4 + j4
                        S.op("pe", lambda e: e.matmul(ps[bk][:, j4 * 128:(j4 + 1) * 128], lhsT=mk[:, kb * 128:(kb + 1) * 128], rhs=ident_bf[:],
                                                      start=True, stop=True), r=[mkb, B("ident")], w=[psb[bk]])
                    S.op("act", lambda e: e.activation(out=mkT[:, kb4 * 512:(kb4 + 1) * 512], in_=ps[bk][:, :], func=AF.Copy),
                         r=[psb[bk]], w=[mkTb])
                for j in range(HKV):
                    sq_ = nq % 2
                    nq += 1
                    load(q_sb[sq_][:], dv(qT, j * G * 128 * T + i * 128, [[T, 128], [128 * T, G], [1, 128]]), qb[sq_], B("qT"))
                    for kgi in range(NKG):
                        s = nkv % 2
                        nkv += 1
                        load(k_sb[s][:], dv(exg, offK + j * 128 * T + kgi * 128, [[T, 128], [EXN, 8], [1, 128]]), kb_[s], B("exg"))
                        load(v_sb[s][:], dv(exg, offV + kgi * 128 * KVW + j * 128, [[KVW, 128], [EXN, 8], [1, 128]]), vb_[s], B("exg"))
                        for cbi in range(8):
                            kb = kgi * 8 + cbi
                            first = (kb == 0)
                            last = (kb == NKG * 8 - 1)
                            bl = nbank % 2
                            nbank += 1
                            s2 = npp % 2
                            npp += 1
                            S.op("pe", lambda e: e.matmul(ps[bl][:, 0:GW], lhsT=k_sb[s][:, cbi, :], rhs=q_sb[sq_][:].rearrange("p g t -> p (g t)"),
                                                          start=True, stop=True), r=[kb_[s], qb[sq_]], w=[psb[bl]])
                            S.op("act", lambda e: e.activation(out=p_sb[s2][:].rearrange("p g t -> p (g t)"), in_=ps[bl][:, 0:GW], func=AF.Exp,
                                                               scale=1.0 / math.sqrt(128.0)), r=[psb[bl]], w=[pb[s2]])
                            S.op("dve", lambda e: e.tensor_tensor(out=pm_sb[s2][:], in0=p_sb[s2][:],
                                                                  in1=mkT[:, kb * 128:(kb + 1) * 128].unsqueeze(1).to_broadcast([128, G, 128]),
                                                                  op=ALU.mult), r=[pb[s2], mkTb], w=[pmb[s2]])
                            S.op("pe", lambda e: e.matmul(ps[6][:, 0:GW], lhsT=v_sb[s][:, cbi, :], rhs=pm_sb[s2][:].rearrange("p g t -> p (g t)"),
                                                          start=first, stop=last), r=[vb_[s], pmb[s2]], w=[psb[6]])
                            S.op("pe", lambda e: e.matmul(ps[7][:, 0:GW], lhsT=ones_bf[:], rhs=pm_sb[s2][:].rearrange("p g t -> p (g t)"),
                                                          start=first, stop=last), r=[B("ones_bf"), pmb[s2]], w=[psb[7]])
                    S.op("dve", lambda e: e.reciprocal(out=rd_sb[:], in_=ps[7][:, 0:GW]), r=[psb[7]], w=[rdb])
                    S.op("dve", lambda e: e.tensor_tensor(out=ya_sb[sq_][:].rearrange("p g t -> p (g t)"), in0=ps[6][:, 0:GW], in1=rd_sb[:],
                                                          op=ALU.mult), r=[psb[6], rdb], w=[yab[sq_]])
                    store(dv(yaT, j * G * 128 * T + i * 128, [[T, 128], [128 * T, G], [1, 128]]), ya_sb[sq_][:], B("yaT"), yab[sq_])
            S.barrier()

        with ExitStack() as st:
            pA_f = sb("p_Af", [128, 8 * 128], F32, st)
            pS_f = sb("p_Sf", [128, 16 * 128], F32, st)
            pw_f = sb("p_wf", [128, 8 * 256], F32, st)
            pA = sb("p_A", [128, 8, 128], BF16, st)
            pS = sb("p_S", [128, 16, 128], BF16, st)
            pw = sb("p_w", [128, 8, 256], BF16, st)
            psc = sb("p_sc", [128, 8], F32, st)
            ct = [sb(f"p_ct{i}", [128, PW], BF16, st) for i in range(2)]
            tA = [sb(f"p_tA{i}", [128, PW], BF16, st) for i in range(2)]
            tB = [sb(f"p_tB{i}", [128, PW], BF16, st) for i in range(2)]
            yp = sb("p_yp", [128, 8, 128], BF16, st)
            yc = [sb(f"p_yc{i}", [128, 8, 128], BF16, st) for i in range(2)]
            pAfb, pSfb, pwfb, pAb, pSb, pwb, pscb, ypb = (B(n) for n in ("p_Af", "p_Sf", "p_wf", "p_A", "p_S", "p_w", "p_sc", "p_yp"))
            ctb = [B("p_ct0"), B("p_ct1")]
            tAb = [B("p_tA0"), B("p_tA1")]
            tBb = [B("p_tB0"), B("p_tB1")]
            ycb = [B("p_yc0"), B("p_yc1")]
            load(pA_f[:], tab_poolA.ap(), pAfb, B("in_s"))
            load(pS_f[:], tab_poolS.ap(), pSfb, B("in_s"))
            load(pw_f[:], pool_w.ap()[l * 128:(l + 1) * 128, :], pwfb, B("in_s"))
            load(psc[:], pool_scale.ap()[l * 128:(l + 1) * 128, :], pscb, B("in_s"))
            S.op("dve", lambda e: e.tensor_copy(out=pA[:].rearrange("p a b -> p (a b)"), in_=pA_f[:]), r=[pAfb], w=[pAb])
            S.op("dve", lambda e: e.tensor_copy(out=pS[:].rearrange("p a b -> p (a b)"), in_=pS_f[:]), r=[pSfb], w=[pSb])
            S.op("dve", lambda e: e.tensor_copy(out=pw[:].rearrange("p a b -> p (a b)"), in_=pw_f[:]), r=[pwfb], w=[pwb])
            bank = 0
            for i in range(NS):
                s = i % 2
                cls = 0 if i == 0 else 1
                ia = max(i - 1, 0)
                load(ct[s][:], dv(cmine, i * 128 * PW, [[PW, 128], [1, PW]]), ctb[s], B("cmine"))
                for cc_ in range(NCORE):
                    load(tA[s][cc_ * 16:(cc_ + 1) * 16, :], dv(exg, cc_ * EXN + offC + ia * 16 * PW, [[PW, 16], [1, PW]]), tAb[s], B("exg"))
                    load(tB[s][cc_ * 16:(cc_ + 1) * 16, :], dv(exg, cc_ * EXN + offC + i * 16 * PW, [[PW, 16], [1, PW]]), tBb[s], B("exg"))
                for dc in range(8):
                    g = dc // 2
                    bk = bank % 8
                    bank += 1
                    dsl = slice(dc * 128, (dc + 1) * 128)
                    S.op("pe", lambda e: e.matmul(ps[bk][:, 0:128], lhsT=ct[s][:, dsl], rhs=pA[:, cls * 4 + g, :], start=True, stop=False),
                         r=[ctb[s], pAb], w=[psb[bk]])
                    S.op("pe", lambda e: e.matmul(ps[bk][:, 0:128], lhsT=tA[s][:, dsl], rhs=pS[:, (cls * 4 + g) * 2, :], start=False, stop=False),
                         r=[tAb[s], pSb], w=[psb[bk]])
                    S.op("pe", lambda e: e.matmul(ps[bk][:, 0:128], lhsT=tB[s][:, dsl], rhs=pS[:, (cls * 4 + g) * 2 + 1, :], start=False, stop=True),
                         r=[tBb[s], pSb], w=[psb[bk]])
                    S.op("act", lambda e: e.activation(out=yp[:, dc, :], in_=ps[bk][:, 0:128], func=AF.Copy), r=[psb[bk]], w=[ypb])
                for g in range(4):
                    for oc in range(2):
                        bk = bank % 8
                        bank += 1
                        for ic in range(2):
                            S.op("pe", lambda e: e.matmul(ps[bk][:, 0:128], lhsT=pw[:, g * 2 + ic, oc * 128:(oc + 1) * 128], rhs=yp[:, g * 2 + ic, :],
                                                          start=(ic == 0), stop=(ic == 1)), r=[pwb, ypb], w=[psb[bk]])
                        S.op("dve", lambda e: e.tensor_scalar(out=yc[s][:, g * 2 + oc, :], in0=ps[bk][:, 0:128], scalar1=psc[:, g * 2 + oc:g * 2 + oc + 1],
                                                              scalar2=None, op0=ALU.mult), r=[psb[bk], pscb], w=[ycb[s]])
                store(dv(ycT, i * 128, [[T, 128], [128 * T, 8], [1, 128]]), yc[s][:], B("ycT"), ycb[s])
            S.barrier()

        with ExitStack() as st:
            XN = sb("XN", [128, KD, T], BF16, st)
            Y = sb("y_Y", [128, (AW + SW + PW) // 128, T], BF16, st)
            Yb = B("y_Y")
            for ci in range(KD):
                load(XN[:, ci, :], hview(nT, ci), XNb, B("nT"))
            srcs = [(yaT, AW // 128, "yaT"), (ybT, SW // 128, "ybT"), (ycT, PW // 128, "ycT")]
            yo = 0
            yoff = []
            for (tt, nchunk, nm) in srcs:
                yoff.append(yo)
                for ci in range(nchunk):
                    load(Y[:, yo + ci, :], hview(tt, ci), Yb, B(nm))
                yo += nchunk
            sgm = [sb(f"y_sg{i}", [128, TN], F32, st) for i in range(2)]
            macc = [sb(f"y_m{i}", [128, TN], F32, st) for i in range(2)]
            mt = sb("y_mt", [128, TN], F32, st)
            ob = [sb(f"y_ob{i}", [128, T], BF16, st) for i in range(2)]
            sgmb = [B("y_sg0"), B("y_sg1")]
            maccb = [B("y_m0"), B("y_m1")]
            mtb = B("y_mt")
            obb = [B("y_ob0"), B("y_ob1")]
            branches = [("ga", (l, "proj_a"), AW // 128, 0), ("gb", (l, "proj_b"), SW // 128, 1), ("gc", (l, "proj_c"), PW // 128, 2)]
            n = 0
            nb = 0
            for ci in range(KD):
                so = ci % 2
                for (gname, pkey, pk, bi) in branches:
                    wg, wgb = wslot()
                    load(wg[:, 0:KD, :], wview(kw, 0, KD, OFF[gname] + ci * 128, 128), wgb, WB(kw))
                    wp, wpb = wslot()
                    load(wp[:, 0:pk, :], wview(pkey, 0, pk, ci * 128, 128), wpb, WB(pkey))
                    par = (nb % 2) * 4
                    nb += 1
                    for tc in range(NTC):
                        for kc in range(KD):
                            S.op("pe", lambda e: e.matmul(ps[par + tc][:, 0:TN], lhsT=wg[:, kc, :], rhs=XN[:, kc, tc * TN:(tc + 1) * TN],
                                                          start=(kc == 0), stop=(kc == KD - 1)), r=[wgb, XNb], w=[psb[par + tc]])
                        for kc in range(pk):
                            S.op("pe", lambda e: e.matmul(ps[par + 2 + tc][:, 0:TN], lhsT=wp[:, kc, :], rhs=Y[:, yoff[bi] + kc, tc * TN:(tc + 1) * TN],
                                                          start=(kc == 0), stop=(kc == pk - 1)), r=[wpb, Yb], w=[psb[par + 2 + tc]])
                    for tc in range(NTC):
                        s2 = n % 2
                        n += 1
                        tsl = slice(tc * TN, (tc + 1) * TN)
                        S.op("act", lambda e: e.activation(out=sgm[s2][:], in_=ps[par + tc][:, 0:TN], func=AF.Sigmoid),
                             r=[psb[par + tc]], w=[sgmb[s2]])
                        if bi == 0:
                            S.op("dve", lambda e: e.tensor_tensor(out=macc[tc][:], in0=sgm[s2][:], in1=ps[par + 2 + tc][:, 0:TN], op=ALU.mult),
                                 r=[sgmb[s2], psb[par + 2 + tc]], w=[maccb[tc]])
                        else:
                            S.op("dve", lambda e: e.tensor_tensor(out=mt[:], in0=sgm[s2][:], in1=ps[par + 2 + tc][:, 0:TN], op=ALU.mult),
                                 r=[sgmb[s2], psb[par + 2 + tc]], w=[mtb])
                            if bi == 1:
                                S.op("dve", lambda e: e.tensor_tensor(out=macc[tc][:], in0=macc[tc][:], in1=mt[:], op=ALU.add),
                                     r=[maccb[tc], mtb], w=[maccb[tc]])
                            else:
                                S.op("dve", lambda e: e.tensor_tensor(out=ob[so][:, tsl], in0=macc[tc][:], in1=mt[:], op=ALU.add),
                                     r=[maccb[tc], mtb], w=[obb[so]])
                store(hview(aT, ci), ob[so][:], B("aT"), obb[so])
            S.barrier()
        with ExitStack() as st:
            emit_outproj(aT, KD, (l, "w_out"), 1.0, st, B("aT"))
            S.barrier()

    for l in range(L):
        emit_ffn(l, 1)
        emit_mixer(l)
        emit_ffn(l, 2)
    with ExitStack() as st:
        XN = sb("XN", [128, KD, T], BF16, st)
        emit_norm(3 * L, st, XN)
        of = [sb(f"z_of{i}", [128, T], F32, st) for i in range(2)]
        ofb = [B("z_of0"), B("z_of1")]
        outb = B("outT")
        for ci in range(KD):
            s = ci % 2
            pass
        hst = [sb(f"z_h{i}", [128, T], F32, st) for i in range(2)]
        hstb = [B("z_h0"), B("z_h1")]
        for ci in range(KD):
            s = ci % 2
            load(hst[s][:], hview(hT, ci), hstb[s], hb[ci])
            gcol = gains_sb[:, 3 * L * KD + ci:3 * L * KD + ci + 1]
            S.op("dve", lambda e: e.scalar_tensor_tensor(out=of[s][:], in0=hst[s][:], scalar=gcol, in1=rstd[:], op0=ALU.mult, op1=ALU.mult),
                 r=[hstb[s], B("rstd"), cb], w=[ofb[s]])
            store(hview(outT, ci), of[s][:], outb, ofb[s])
        S.barrier()
    es.close()
    return nc, c


def host_tables(c, core):
    NS = c["NS"]
    t = {}
    rope = np.zeros((128, 4), np.float32)
    for d in range(128):
        if d < 32:
            rope[d, 0] = ROPE_THETA ** (-(2.0 * (d % 16)) / 32.0)
            rope[d, 1] = -1.0 if d < 16 else 1.0
        dd = d % 64
        if dd < 16:
            rope[d, 2] = ROPE_THETA ** (-(2.0 * (dd % 8)) / 16.0)
            rope[d, 3] = -1.0 if dd < 8 else 1.0
    t["tab_rope"] = rope
    P = np.zeros((128, 256), np.float32)
    for m in range(32):
        kk = m + 16 if m < 16 else m - 16
        P[kk, m] = 1.0
    for m in range(128):
        dd = m % 64
        if dd < 16:
            kk = m + 8 if dd < 8 else m - 8
            P[kk, 128 + m] = 1.0
    t["tab_P"] = P
    t["tab_ident"] = np.eye(128, dtype=np.float32)
    jj, ii = np.meshgrid(np.arange(128), np.arange(128), indexing="ij")
    t["tab_tril"] = (jj <= ii).astype(np.float32)
    tt, col = np.meshgrid(np.arange(128), np.arange(1024), indexing="ij")
    t["tab_pen"] = np.where(col > 128 * core + tt, -BIG, 0.0).astype(np.float32)
    A = np.zeros((128, 2, 4, 128), np.float32)
    Sel = np.zeros((128, 2, 4, 2, 128), np.float32)
    for cls in range(2):
        first = (cls == 0 and core == 0)
        for g, w in enumerate(POOL_WINDOWS):
            for tq in range(128):
                cnt = min(w, tq + 1) if first else w
                for d_ in range(w):
                    s = tq - d_
                    if s >= 0:
                        A[s, cls, g, tq] += 1.0 / cnt
                    elif not first:
                        r = 16 + s
                        if core >= 1:
                            Sel[(core - 1) * 16 + r, cls, g, 1, tq] += 1.0 / cnt
                        elif cls == 1:
                            Sel[7 * 16 + r, cls, g, 0, tq] += 1.0 / cnt
                A[tq, cls, g, tq] -= 1.0
    t["tab_poolA"] = A.reshape(128, -1)
    t["tab_poolS"] = Sel.reshape(128, -1)
    return t


def prepare_inputs(c, inputs):
    D, T, L, KD, NS, SG = c["D"], c["T"], c["L"], c["KD"], c["NS"], c["SG"]
    x = np.asarray(inputs["x"])[0]
    positions = np.asarray(inputs["positions"])[0]
    wmap = {"ffn1_in": "ffn1_w_in", "ffn1_out": "ffn1_w_out", "w_in": "w_in", "proj_a": "proj_a", "proj_b": "proj_b",
            "proj_c": "proj_c", "w_out": "w_out", "ffn2_in": "ffn2_w_in", "ffn2_out": "ffn2_w_out"}

    def fm(v):
        return np.ascontiguousarray(np.asarray(v).reshape(-1, 128).T)

    gl = []
    for l in range(L):
        gl += [fm(inputs["ffn1_norm"][l]), fm(inputs["mix_norm"][l]), fm(inputs["ffn2_norm"][l])]
    gl.append(fm(inputs["final_norm"]))
    gains = np.ascontiguousarray(np.concatenate(gl, axis=1)).astype(np.float32)
    sgu_w = np.asarray(inputs["sgu_w"])
    sgu_wT = np.ascontiguousarray(np.transpose(sgu_w, (0, 3, 1, 2)).reshape(L * 128, SG * 128))
    sgu_b = np.ascontiguousarray(np.asarray(inputs["sgu_b"]).reshape(L, SG * 128))
    pw = np.asarray(inputs["pool_w"])
    pool_w = np.ascontiguousarray(pw.reshape(L, 4, 2, 128, 256).transpose(0, 3, 1, 2, 4).reshape(L * 128, 8 * 256))
    pool_scale = np.ascontiguousarray(np.asarray(inputs["pool_scale"]).reshape(L, 8, 128).transpose(0, 2, 1).reshape(L * 128, 8))
    common = {"gains": gains, "sgu_lng": np.ascontiguousarray(inputs["sgu_ln_g"]), "sgu_lnb": np.ascontiguousarray(inputs["sgu_ln_b"]),
              "sgu_wT": sgu_wT, "sgu_b": sgu_b, "pool_w": pool_w, "pool_scale": pool_scale}
    in_maps = []
    for core in range(NCORE):
        m = dict(common)
        tiles = [8 * i + core for i in range(NS)]
        rows = np.concatenate([np.arange(g * 128, (g + 1) * 128) for g in tiles])
        m["xT"] = np.ascontiguousarray(x[rows].T)
        m["pos"] = np.ascontiguousarray(positions[rows][None, :]).astype(np.int32)
        for l in range(L):
            for n in WNAMES:
                w = np.asarray(inputs[wmap[n]][l])
                K_ = w.shape[0]
                sh = w[core * (K_ // NCORE):(core + 1) * (K_ // NCORE)]
                m[f"w{l}_{n}"] = np.ascontiguousarray(sh).reshape(128, -1)
        m.update(host_tables(c, core))
        in_maps.append(m)
    return in_maps


def assemble(c, results):
    D, T, NS = c["D"], c["T"], c["NS"]
    out = np.zeros((NCORE * T, D), np.float32)
    for core in range(NCORE):
        o = np.asarray(results[core]["outT"]).T
        for i in range(NS):
            g = 8 * i + core
            out[g * 128:(g + 1) * 128] = o[i * 128:(i + 1) * 128]
    return out[None]


def run(cfg, inputs):
    nc, c = build(cfg)
    in_maps = prepare_inputs(c, inputs)
    res = run_bass_kernel_spmd(nc, in_maps, core_ids=list(range(NCORE)))
    return assemble(c, res.results)


def kernel(**inputs):
    return run(FULL_CFG, inputs)
```
